# Optimizing a Trainium2 kernel written in Bass

```python
import jax, jax.numpy as jnp
from jax import lax
import numpy as np

D_MODEL = 1024
BATCH = 8
SEQ = 4096
DEPTH = 4

N_MIXERS = 2
N_A = (DEPTH + 1) // 2
N_B = DEPTH // 2
WIDTH = 3 * D_MODEL // 2
LRU_HEADS = 12
LRU_BLOCK = WIDTH // LRU_HEADS
CONV_A = 4
CONV_B = 3
LRU_C = 8.0
NORM_EPS = 1e-6

kernel_name = "hybrid_rglru_shortconv_trunk"


def rmsnorm(x, g):
    xf = x.astype(jnp.float32)
    inv = lax.rsqrt(jnp.mean(xf * xf, axis=-1, keepdims=True) + NORM_EPS)
    return (xf * inv * g.astype(jnp.float32)).astype(x.dtype)


def causal_depthwise_conv(u, w, b=None):
    k_width = w.shape[0]
    s = u.shape[1]
    up = jnp.pad(u, ((0, 0), (k_width - 1, 0), (0, 0)))
    out = up[:, 0:s] * w[0]
    for k in range(1, k_width):
        out = out + up[:, k:k + s] * w[k]
    if b is not None:
        out = out + b
    return out


def rg_lru(u, gate_w, gate_b, lam):
    bsz, s, w = u.shape
    uh = u.reshape(bsz, s, LRU_HEADS, LRU_BLOCK)
    g = jnp.einsum('bshi,hio->bsho', uh, gate_w) + gate_b
    g = jax.nn.sigmoid(g.astype(jnp.float32))
    r = g[..., :LRU_BLOCK].reshape(bsz, s, w)
    i = g[..., LRU_BLOCK:].reshape(bsz, s, w)
    log_a = -LRU_C * r * jax.nn.softplus(-lam.astype(jnp.float32))
    a = jnp.exp(log_a)
    mult = jnp.sqrt(-jnp.expm1(2.0 * log_a))
    bterm = mult * (i * u.astype(jnp.float32))

    def combine(left, right):
        a_l, b_l = left
        a_r, b_r = right
        return a_l * a_r, a_r * b_l + b_r

    _, h = lax.associative_scan(combine, (a, bterm), axis=1)
    return h.astype(u.dtype)


def recurrent_mixer(h, w_in, conv_w, conv_b, gate_w, gate_b, lam, w_out):
    u = jnp.einsum('bsd,dc->bsc', h, w_in)
    gate, xb = u[..., :WIDTH], u[..., WIDTH:]
    xb = causal_depthwise_conv(xb, conv_w, conv_b)
    y = rg_lru(xb, gate_w, gate_b, lam)
    return jnp.einsum('bsc,cd->bsd', y * jax.nn.silu(gate), w_out)


def shortconv_mixer(h, w_in, conv_w, w_out):
    u = jnp.einsum('bsd,dc->bsc', h, w_in)
    bg = u[..., :WIDTH]
    cg = u[..., WIDTH:2 * WIDTH]
    xv = u[..., 2 * WIDTH:3 * WIDTH]
    gate = u[..., 3 * WIDTH:]
    y = bg * causal_depthwise_conv(cg * xv, conv_w)
    return jnp.einsum('bsc,cd->bsd', y * jax.nn.silu(gate), w_out)


def setup_inputs(seed: int = 0) -> dict:
    key = jax.random.key(seed)
    ks = jax.random.split(key, 16)
    f32 = jnp.float32
    x = jax.random.normal(ks[0], (BATCH, SEQ, D_MODEL), f32)
    norm_g = 1.0 + 0.02 * jax.random.normal(ks[1], (DEPTH, D_MODEL), f32)
    a_w_in = jax.random.normal(ks[2], (N_A, D_MODEL, 2 * WIDTH), f32) * D_MODEL ** -0.5
    a_conv_w = jax.random.normal(ks[3], (N_A, CONV_A, WIDTH), f32) * CONV_A ** -0.5
    a_conv_b = 0.02 * jax.random.normal(ks[4], (N_A, WIDTH), f32)
    a_gate_w = jax.random.normal(ks[5], (N_A, LRU_HEADS, LRU_BLOCK, 2 * LRU_BLOCK), f32) * LRU_BLOCK ** -0.5
    a_gate_b = 0.02 * jax.random.normal(ks[6], (N_A, LRU_HEADS, 2 * LRU_BLOCK), f32)
    a0 = jax.random.uniform(ks[7], (N_A, WIDTH), f32, minval=0.9, maxval=0.999)
    a_lambda = jnp.log(a0) - jnp.log1p(-a0)
    a_w_out = jax.random.normal(ks[8], (N_A, WIDTH, D_MODEL), f32) * WIDTH ** -0.5
    b_w_in = jax.random.normal(ks[9], (N_B, D_MODEL, 4 * WIDTH), f32) * D_MODEL ** -0.5
    b_conv_w = jax.random.normal(ks[10], (N_B, CONV_B, WIDTH), f32) * CONV_B ** -0.5
    b_w_out = jax.random.normal(ks[11], (N_B, WIDTH, D_MODEL), f32) * WIDTH ** -0.5
    final_g = 1.0 + 0.02 * jax.random.normal(ks[12], (D_MODEL,), f32)
    return {"x": x, "norm_g": norm_g, "a_w_in": a_w_in, "a_conv_w": a_conv_w,
            "a_conv_b": a_conv_b, "a_gate_w": a_gate_w, "a_gate_b": a_gate_b,
            "a_lambda": a_lambda, "a_w_out": a_w_out, "b_w_in": b_w_in,
            "b_conv_w": b_conv_w, "b_w_out": b_w_out, "final_g": final_g}


def reference(x, norm_g, a_w_in, a_conv_w, a_conv_b, a_gate_w, a_gate_b, a_lambda,
              a_w_out, b_w_in, b_conv_w, b_w_out, final_g):
    for i in range(DEPTH):
        h = rmsnorm(x, norm_g[i])
        j = i // N_MIXERS
        if i % N_MIXERS == 0:
            x = x + recurrent_mixer(h, a_w_in[j], a_conv_w[j], a_conv_b[j], a_gate_w[j],
                                    a_gate_b[j], a_lambda[j], a_w_out[j])
        else:
            x = x + shortconv_mixer(h, b_w_in[j], b_conv_w[j], b_w_out[j])
    return rmsnorm(x, final_g)
```

```python
import numpy as np
import concourse.bass as bass
import concourse.mybir as mybir
from concourse.bass_utils import run_bass_kernel_spmd

F32 = mybir.dt.float32
BF16 = mybir.dt.bfloat16
ALU = mybir.AluOpType
AF = mybir.ActivationFunctionType
P = 128
NORM_EPS = 1e-6


class Cfg:
    def __init__(self, D=1024, H=12, S=4096, T=1024, NS=512, layers=("A0", "B0", "A1", "B1"),
                 final_norm=True, NA=2, NB=2, DEPTH=4, GA=6, GB=4, NWS=3):
        self.D, self.H, self.S, self.T, self.NS = D, H, S, T, NS
        self.W = H * P
        self.KC = D // P
        self.layers = tuple(layers)
        self.final_norm = final_norm
        self.NA, self.NB, self.DEPTH = NA, NB, DEPTH
        self.NSUB = T // NS
        self.NT = S // T
        self.IA = 2 * self.KC * P + 2 * P
        self.IB = 4 * self.KC * P
        self.CA = self.IA + D
        self.CB = self.IB + D
        self.NWS = NWS
        self.GA, self.GB = GA, GB
        o = 0
        self.o_ng = o; o += DEPTH * self.KC
        self.o_fg = o; o += self.KC
        self.o_acw = o; o += NA * 4 * H
        self.o_acb = o; o += NA * H
        self.o_agb = o; o += NA * 2 * H
        self.o_lam = o; o += NA * H
        self.o_bcw = o; o += NB * 3 * H
        self.NPV = o


class Buf:
    __slots__ = ("name", "w", "r", "owner")

    def __init__(self, name):
        self.name = name
        self.w = None
        self.r = {}
        self.owner = None


class Eng:
    def __init__(self, nc, name):
        self.name = name
        self.h = getattr(nc, name)
        self.sem = nc.alloc_semaphore("s_" + name)
        self.count = 0
        self.waited = {}


class Prog:
    def __init__(self, nc):
        self.nc = nc
        self.pe = Eng(nc, "tensor")
        self.act = Eng(nc, "scalar")
        self.dve = Eng(nc, "vector")
        self.pool = Eng(nc, "gpsimd")
        self.sp = Eng(nc, "sync")
        self.engs = [self.pe, self.act, self.dve, self.pool, self.sp]
        self.dma_counts = {}
        self.n_inst = 0

    def _wait(self, eng, deps):
        for sem, val in deps.items():
            if eng.waited.get(sem, 0) >= val:
                continue
            eng.h.wait_ge(sem, val)
            eng.waited[sem] = val

    @staticmethod
    def _deps(reads, writes):
        deps = {}

        def add(ev):
            if ev is not None:
                s, v = ev
                if deps.get(s, 0) < v:
                    deps[s] = v
        for b in reads:
            add(b.w)
        for b in writes:
            add(b.w)
            for s, v in b.r.items():
                add((s, v))
        return deps

    @staticmethod
    def _mark(ev, reads, writes):
        s, v = ev
        for b in reads:
            if b.r.get(s, 0) < v:
                b.r[s] = v
        for b in writes:
            b.w = ev
            b.r = {}

    def op(self, eng, fn, reads=(), writes=()):
        self._wait(eng, self._deps(reads, writes))
        ins = fn(eng.h)
        eng.count += 1
        ins.then_inc(eng.sem, 1)
        ev = (eng.sem, eng.count)
        self._mark(ev, reads, writes)
        self.n_inst += 1
        return ev

    def ops(self, eng, fns, reads=(), writes=()):
        self._wait(eng, self._deps(reads, writes))
        ins = None
        for fn in fns:
            ins = fn(eng.h)
            self.n_inst += 1
        eng.count += 1
        ins.then_inc(eng.sem, 1)
        ev = (eng.sem, eng.count)
        self._mark(ev, reads, writes)
        return ev

    def dma(self, eng, sem, out, in_, reads=(), writes=()):
        self._wait(eng, self._deps(reads, writes))
        ins = eng.h.dma_start(out=out, in_=in_)
        c = self.dma_counts.get(sem, 0) + 16
        self.dma_counts[sem] = c
        ins.then_inc(sem, 16)
        ev = (sem, c)
        self._mark(ev, reads, writes)
        self.n_inst += 1
        return ev

    def barrier_wait_all(self, eng, bufs):
        self._wait(eng, self._deps((), bufs))


class Pool:
    def __init__(self, name, aps):
        self.name = name
        self.aps = aps
        self.bufs = [Buf(f"{name}{i}") for i in range(len(aps))]
        self.i = 0

    def get(self, owner):
        k = self.i % len(self.aps)
        self.i += 1
        b = self.bufs[k]
        assert b.owner is None, f"pool {self.name} entry {k} still owned by {b.owner} (wanted by {owner})"
        b.owner = owner
        return self.aps[k], b

    @staticmethod
    def rel(*bufs):
        for b in bufs:
            b.owner = None


def build_program(cfg: Cfg):
    c = cfg
    D, H, S, T, NS, KC, NSUB, NT = c.D, c.H, c.S, c.T, c.NS, c.KC, c.NSUB, c.NT
    nc = bass.Bass("TRN2", target_bir_lowering=False)
    pg = Prog(nc)
    pe, act, dve, pool, sp = pg.pe, pg.act, pg.dve, pg.pool, pg.sp

    xT = nc.dram_tensor("xT", [D, S], F32, kind="ExternalInput").ap()
    pv_d = nc.dram_tensor("pv", [P, c.NPV], F32, kind="ExternalInput").ap()
    wa_d = nc.dram_tensor("wa", [c.NA, H, P, c.CA], F32, kind="ExternalInput").ap()
    wb_d = nc.dram_tensor("wb", [c.NB, H, P, c.CB], F32, kind="ExternalInput").ap()
    oT = nc.dram_tensor("oT", [D, S], F32, kind="ExternalOutput").ap()
    wa16 = nc.dram_tensor("wa16", [c.NA, H, P, c.CA], BF16).ap()
    wb16 = nc.dram_tensor("wb16", [c.NB, H, P, c.CB], BF16).ap()
    xT_v = xT.rearrange("(k p) s -> p k s", p=P)
    oT_v = oT.rearrange("(k p) s -> p k s", p=P)

    def sb(name, shape, dt):
        return nc.alloc_sbuf_tensor(name, shape, dt)

    x_sb = sb("x_sb", [P, KC, T], F32)
    h_sb = sb("h_sb", [P, KC, T], BF16)
    y_sb = sb("y_sb", [P, H, T], BF16)
    CW = max(c.IA, c.IB)
    NWS = c.NWS
    w_sb = [sb(f"w_sb{i}", [P, CW], BF16) for i in range(NWS)]
    wo_sb = sb("wo_sb", [P, H, D], BF16)
    pv = sb("pv_sb", [P, c.NPV], F32)
    NAH = c.NA * H
    hb = sb("hb", [P, 2 * NAH], F32)
    ca = sb("ca", [P, NAH], F32)
    c2 = sb("c2", [P, NAH], F32)
    tmpv = sb("tmpv", [P, 4 * NAH], F32)
    cst = sb("cst", [P, 4], F32)
    ones = sb("ones", [P, P], BF16)
    sq_sb = sb("sq_sb", [P, KC, NS], BF16)
    rs_sb = [sb(f"rs_sb{i}", [P, NS], F32) for i in range(2)]
    st_xa = sb("st_xa", [P, c.NA, H, 3], F32)
    st_h = sb("st_h", [P, c.NA, H, 1], F32)
    st_pb = sb("st_pb", [P, c.NB, H, 2], F32)
    NHB = 2
    xbuf = [sb(f"xbuf{i}", [P, 4 + T], F32) for i in range(NHB)]
    hbuf = [sb(f"hbuf{i}", [P, 4 + T], F32) for i in range(NHB)]
    def mkpool(name, n, dt=F32):
        return Pool(name, [sb(f"{name}{i}", [P, NS], dt)[:, :] for i in range(n)])

    p_tg = mkpool("tg", 2)
    p_sg = mkpool("sg", 6)
    p_xc = mkpool("xc", 3)
    p_xcb = mkpool("xcb", 3, BF16)
    p_tr = mkpool("tr", 4)
    p_ti = mkpool("ti", 4)
    p_a = mkpool("a", 4)
    p_xv = p_tr
    p_bg = p_ti
    p_cc = p_a

    ps = nc.alloc_psum_tensor("ps", [P, 8, 512], F32)
    banks = [Buf(f"bank{i}") for i in range(8)]

    def bank_ap(i):
        return ps[:, i, 0:NS]

    class PairPool:
        def __init__(self, pairs):
            self.pairs = pairs
            self.i = 0

        def get(self, owner):
            pr = self.pairs[self.i % len(self.pairs)]
            self.i += 1
            for b in pr:
                assert banks[b].owner is None, f"bank {b} owned by {banks[b].owner}, wanted {owner}"
                banks[b].owner = owner
            return pr

    s_par = nc.alloc_semaphore("s_par")
    s_cast = {}
    s_w = [nc.alloc_semaphore(f"s_w{i}") for i in range(NWS)]
    s_wo = nc.alloc_semaphore("s_wo")
    s_xin = nc.alloc_semaphore("s_xin")
    s_out = [nc.alloc_semaphore(f"s_out{k}") for k in range(KC)]

    b_pv = Buf("pv")
    b_par = Buf("par")
    b_const = Buf("const")
    b_x = [[Buf(f"x{k}_{s}") for s in range(NSUB)] for k in range(KC)]
    b_h = [Buf(f"h{s}") for s in range(NSUB)]
    b_y = [[Buf(f"y{h}_{s}") for s in range(NSUB)] for h in range(H)]
    b_w = [Buf(f"w{i}") for i in range(NWS)]
    b_wo = Buf("wo")
    b_sq = Buf("sq")
    b_rs = [Buf("rs0"), Buf("rs1")]
    b_st_xa = [[Buf(f"stxa{j}_{h}") for h in range(H)] for j in range(c.NA)]
    b_st_h = [[Buf(f"sth{j}_{h}") for h in range(H)] for j in range(c.NA)]
    b_st_pb = [[Buf(f"stpb{j}_{h}") for h in range(H)] for j in range(c.NB)]
    b_xbuf = [[Buf(f"xbuf{i}_{s}") for s in range(NSUB)] for i in range(NHB)]
    b_hbuf = [[Buf(f"hbuf{i}_{s}") for s in range(NSUB)] for i in range(NHB)]
    b_cast = {}

    pg.dma(sp, s_par, pv[:, :], pv_d, writes=[b_pv])

    def ms(ap, val):
        return lambda e: e.memset(ap, val)
    pg.op(dve, ms(cst[:, 0:1], NORM_EPS), writes=[b_const])
    pg.op(dve, ms(cst[:, 1:2], 1.0), writes=[b_const])
    pg.op(dve, ms(cst[:, 2:3], 0.0), writes=[b_const])
    pg.op(dve, ms(ones[:, :], 1.0), writes=[b_const])
    b_state = Buf("state_init")
    pg.op(dve, ms(st_xa[:, :, :, :], 0.0), writes=[b_state])
    pg.op(dve, ms(st_h[:, :, :, :], 0.0), writes=[b_state])
    pg.op(dve, ms(st_pb[:, :, :, :], 0.0), writes=[b_state])
    for j in range(c.NA):
        for h in range(H):
            b_st_xa[j][h].w = b_state.w
            b_st_h[j][h].w = b_state.w
    for j in range(c.NB):
        for h in range(H):
            b_st_pb[j][h].w = b_state.w

    used = []
    for L in c.layers:
        if L not in used:
            used.append(L)
    for L in used:
        kind, j = L[0], int(L[1:])
        s_cast[L] = nc.alloc_semaphore("s_cast" + L)
        b_cast[L] = Buf("cast" + L)
        src, dst = (wa_d, wa16) if kind == "A" else (wb_d, wb16)
        for h in range(H):
            pg.dma(pool, s_cast[L], dst[j, h], src[j, h], writes=[])
        b_cast[L].w = (s_cast[L], pg.dma_counts[s_cast[L]])

    lam = pv[:, c.o_lam:c.o_lam + NAH]
    t_nl, t_ab, t_e, t_l = (tmpv[:, i * NAH:(i + 1) * NAH] for i in range(4))
    b_tmpv = Buf("tmpv")
    pg.op(dve, lambda e: e.tensor_scalar(hb[:, :], pv[:, c.o_agb:c.o_agb + 2 * NAH], 0.5, None, op0=ALU.mult),
          reads=[b_pv], writes=[b_par])
    pg.op(dve, lambda e: e.tensor_scalar(t_nl, lam, -1.0, None, op0=ALU.mult), reads=[b_pv], writes=[b_tmpv])
    pg.op(dve, lambda e: e.tensor_tensor(t_ab, lam, t_nl, op=ALU.max), reads=[b_pv, b_tmpv], writes=[b_tmpv])
    pg.op(act, lambda e: e.activation(t_e, t_ab, AF.Exp, bias=cst[:, 2:3], scale=-1.0),
          reads=[b_tmpv, b_const], writes=[b_tmpv])
    pg.op(act, lambda e: e.activation(t_l, t_e, AF.Ln, bias=cst[:, 1:2], scale=1.0),
          reads=[b_tmpv, b_const], writes=[b_tmpv])
    pg.op(dve, lambda e: e.tensor_scalar(t_nl, t_nl, 0.0, None, op0=ALU.max), reads=[b_tmpv], writes=[b_tmpv])
    pg.op(dve, lambda e: e.tensor_tensor(t_l, t_l, t_nl, op=ALU.add), reads=[b_tmpv], writes=[b_tmpv])
    pg.op(dve, lambda e: e.tensor_scalar(ca[:, :], t_l, -4.0, None, op0=ALU.mult), reads=[b_tmpv], writes=[b_par])
    pg.op(dve, lambda e: e.tensor_scalar(c2[:, :], t_l, -8.0, None, op0=ALU.mult), reads=[b_tmpv], writes=[b_par])

    def try_load_weights(L, h):
        kind, j = L[0], int(L[1:])
        for i in range(NWS):
            if b_w[i].owner is None:
                break
        else:
            return None
        b_w[i].owner = (L, h)
        src = wa16 if kind == "A" else wb16
        ncol = c.IA if kind == "A" else c.IB
        pg.dma(sp, s_w[i], w_sb[i][:, 0:ncol], src[j, h, :, 0:ncol], reads=[b_cast[L]], writes=[b_w[i]])
        return i

    def load_wout(L):
        kind, j = L[0], int(L[1:])
        src = wa16 if kind == "A" else wb16
        o = c.IA if kind == "A" else c.IB
        pg.dma(sp, s_wo, wo_sb[:, :, :], src[j, :, :, o:o + D].rearrange("h p d -> p h d"),
               reads=[b_cast[L]], writes=[b_wo])

    class WStream:
        def __init__(self, L):
            self.L = L
            self.widx = {}
            self.next = 0

        def prefetch(self, upto):
            while self.next <= min(upto, H - 1):
                i = try_load_weights(self.L, self.next)
                if i is None:
                    return
                self.widx[self.next] = i
                self.next += 1

        def release(self, h):
            b_w[self.widx[h]].owner = None

    def rmsnorm_stats(s, bank):
        sl = slice(s * NS, (s + 1) * NS)
        xs = [b_x[k][s] for k in range(KC)]
        pg.op(act, lambda e: e.activation(sq_sb[:, :, :], x_sb[:, :, sl], AF.Square),
              reads=xs, writes=[b_sq])
        fns = []
        for k in range(KC):
            fns.append(lambda e, k=k: e.matmul(bank_ap(bank), ones[:, :], sq_sb[:, k, :],
                                               start=(k == 0), stop=(k == KC - 1)))
        pg.ops(pe, fns, reads=[b_sq, b_const], writes=[banks[bank]])
        pg.op(act, lambda e: e.activation(rs_sb[s][:, :], bank_ap(bank), AF.Sqrt, bias=cst[:, 0:1], scale=1.0 / D),
              reads=[banks[bank], b_const], writes=[b_rs[s]])
        pg.op(dve, lambda e: e.reciprocal(rs_sb[s][:, :], rs_sb[s][:, :]), reads=[b_rs[s]], writes=[b_rs[s]])

    def layer_norm_in(li):
        for s in range(NSUB):
            rmsnorm_stats(s, 6 + (s % 2))
        for s in range(NSUB):
            sl = slice(s * NS, (s + 1) * NS)
            for k in range(KC):
                g = pv[:, c.o_ng + li * KC + k: c.o_ng + li * KC + k + 1]
                pg.op(dve, lambda e, k=k, g=g: e.scalar_tensor_tensor(
                    h_sb[:, k, sl], x_sb[:, k, sl], g, rs_sb[s][:, :], op0=ALU.mult, op1=ALU.mult),
                    reads=[b_x[k][s], b_rs[s], b_pv], writes=[b_h[s]])

    oc = {"n": 0}

    def out_proj(heads):
        for s in range(NSUB):
            sl = slice(s * NS, (s + 1) * NS)
            for k in range(KC):
                bank = 6 + (oc["n"] % 2)
                oc["n"] += 1
                fns = []
                for n, hd in enumerate(heads):
                    fns.append(lambda e, hd=hd, n=n: e.matmul(
                        bank_ap(bank), wo_sb[:, hd, k * P:(k + 1) * P], y_sb[:, hd, sl],
                        start=(n == 0), stop=(n == len(heads) - 1)))
                pg.ops(pe, fns, reads=[b_y[hd][s] for hd in heads] + [b_wo], writes=[banks[bank]])
                pg.op(dve, lambda e: e.tensor_tensor(x_sb[:, k, sl], x_sb[:, k, sl], bank_ap(bank), op=ALU.add),
                      reads=[banks[bank]], writes=[b_x[k][s]])

    def layer_A(li, j, ti):
        L = f"A{j}"
        units = [(h, s) for h in range(H) for s in range(NSUB)]
        NU = len(units)
        pp_in = PairPool([(0, 1), (2, 3)])
        pp_g = PairPool([(4, 5)])
        ws = WStream(L)
        widx = ws.widx
        U = [dict() for _ in range(NU)]
        ws.prefetch(0)

        def col(base, h):
            return pv[:, base + h: base + h + 1]

        def PE0(u):
            h, s = units[u]
            assert h in widx, f"weights for head {h} not loaded"
            d = U[u]
            d["inb"] = pp_in.get(("A", u))
            wt = w_sb[widx[h]]
            sl = slice(s * NS, (s + 1) * NS)
            for part in range(2):
                bank = d["inb"][part]
                fns = []
                for k in range(KC):
                    o = (part * KC + k) * P
                    fns.append(lambda e, o=o, k=k, bank=bank: e.matmul(
                        bank_ap(bank), wt[:, o:o + P], h_sb[:, k, sl], start=(k == 0), stop=(k == KC - 1)))
                pg.ops(pe, fns, reads=[b_w[widx[h]], b_h[s]], writes=[banks[bank]])

        def E1(u):
            h, s = units[u]
            d = U[u]
            hbi = h % NHB
            bg_, bx_ = d["inb"]
            xb_ap = xbuf[hbi]
            off = 4 + s * NS
            if s == 0:
                pg.op(act, lambda e: e.copy(xb_ap[:, 1:4], st_xa[:, j, h, :]),
                      reads=[b_st_xa[j][h]], writes=[b_xbuf[hbi][0]])
            pg.op(act, lambda e: e.activation(xb_ap[:, off:off + NS], bank_ap(bx_), AF.Identity,
                                              bias=cst[:, 2:3], scale=1.0),
                  reads=[banks[bx_], b_const], writes=[b_xbuf[hbi][s]])
            if s == NSUB - 1:
                pg.op(act, lambda e: e.copy(st_xa[:, j, h, :], xb_ap[:, 1 + T:4 + T]),
                      reads=[b_xbuf[hbi][s]], writes=[b_st_xa[j][h]])
            tg, btg = p_tg.get(u)
            pg.op(act, lambda e: e.activation(tg, bank_ap(bg_), AF.Tanh, bias=cst[:, 2:3], scale=0.5),
                  reads=[banks[bg_], b_const], writes=[btg])
            sg, bsg = p_sg.get(u)
            pg.op(dve, lambda e: e.scalar_tensor_tensor(sg, tg, 1.0, bank_ap(bg_), op0=ALU.add, op1=ALU.mult),
                  reads=[btg, banks[bg_]], writes=[bsg])
            Pool.rel(btg)
            for b in d["inb"]:
                banks[b].owner = None
            xc, bxc = p_xc.get(u)
            rd = [b_xbuf[hbi][s]] + ([b_xbuf[hbi][s - 1]] if s > 0 else []) + [b_pv]
            cw = lambda k: col(c.o_acw + (j * 4 + k) * H, h)
            pg.op(dve, lambda e: e.tensor_scalar(xc, xb_ap[:, off:off + NS], cw(3), col(c.o_acb + j * H, h),
                                                 op0=ALU.mult, op1=ALU.add), reads=rd, writes=[bxc])
            for k in (2, 1, 0):
                sh = 3 - k
                pg.op(dve, lambda e, k=k, sh=sh: e.scalar_tensor_tensor(
                    xc, xb_ap[:, off - sh:off - sh + NS], cw(k), xc, op0=ALU.mult, op1=ALU.add),
                    reads=rd + [bxc], writes=[bxc])
            xcb, bxcb = p_xcb.get(u)
            pg.op(act, lambda e: e.copy(xcb, xc), reads=[bxc], writes=[bxcb])
            d.update(sg=sg, bsg=bsg, xc=xc, bxc=bxc, xcb=xcb, bxcb=bxcb)

        def PE2(u):
            h, s = units[u]
            d = U[u]
            d["gb"] = pp_g.get(("Ag", u))
            wt = w_sb[widx[h]]
            go = 2 * KC * P
            for part in range(2):
                bank = d["gb"][part]
                pg.ops(pe, [lambda e, part=part, bank=bank: e.matmul(
                    bank_ap(bank), wt[:, go + part * P: go + (part + 1) * P], d["xcb"], start=True, stop=True)],
                    reads=[b_w[widx[h]], d["bxcb"]], writes=[banks[bank]])
            Pool.rel(d["bxcb"])
            if s == NSUB - 1:
                ws.release(h)

        def E3(u):
            h, s = units[u]
            d = U[u]
            br_, bi_ = d["gb"]
            jh = j * H + h
            tr, btr = p_tr.get(u)
            ti_, bti = p_ti.get(u)
            a_, ba = p_a.get(u)
            pg.op(act, lambda e: e.activation(tr, bank_ap(br_), AF.Tanh, bias=hb[:, (j * 2) * H + h:(j * 2) * H + h + 1],
                                              scale=0.5), reads=[banks[br_], b_par], writes=[btr])
            pg.op(act, lambda e: e.activation(ti_, bank_ap(bi_), AF.Tanh,
                                              bias=hb[:, (j * 2 + 1) * H + h:(j * 2 + 1) * H + h + 1], scale=0.5),
                  reads=[banks[bi_], b_par], writes=[bti])
            for b in d["gb"]:
                banks[b].owner = None
            pg.op(act, lambda e: e.activation(a_, tr, AF.Exp, bias=ca[:, jh:jh + 1], scale=ca[:, jh:jh + 1]),
                  reads=[btr, b_par], writes=[ba])
            pg.op(act, lambda e: e.activation(tr, tr, AF.Exp, bias=c2[:, jh:jh + 1], scale=c2[:, jh:jh + 1]),
                  reads=[btr, b_par], writes=[btr])
            pg.op(dve, lambda e: e.scalar_tensor_tensor(ti_, ti_, 1.0, d["xc"], op0=ALU.add, op1=ALU.mult),
                  reads=[bti, d["bxc"]], writes=[bti])
            Pool.rel(d["bxc"])
            d.update(tr=tr, btr=btr, ti=ti_, bti=bti, a=a_, ba=ba)

        def E4(u):
            d = U[u]
            pg.op(act, lambda e: e.activation(d["tr"], d["tr"], AF.Sqrt, bias=cst[:, 1:2], scale=-1.0),
                  reads=[d["btr"], b_const], writes=[d["btr"]])

        def E5(u):
            h, s = units[u]
            d = U[u]
            hbi = h % NHB
            hb_ap = hbuf[hbi]
            off = 4 + s * NS
            sl = slice(s * NS, (s + 1) * NS)
            pg.op(dve, lambda e: e.tensor_tensor(d["ti"], d["ti"], d["tr"], op=ALU.mult),
                  reads=[d["bti"], d["btr"]], writes=[d["bti"]])
            if s == 0:
                pg.op(act, lambda e: e.copy(hb_ap[:, 3:4], st_h[:, j, h, :]),
                      reads=[b_st_h[j][h]], writes=[b_hbuf[hbi][0]])
            rd = [d["ba"], d["bti"]] + ([b_hbuf[hbi][s - 1]] if s > 0 else [])
            pg.op(dve, lambda e: e.tensor_tensor_scan(hb_ap[:, off:off + NS], d["a"], d["ti"], hb_ap[:, off - 1:off],
                                                      op0=ALU.mult, op1=ALU.add),
                  reads=rd + [b_hbuf[hbi][s]], writes=[b_hbuf[hbi][s]])
            if s == NSUB - 1:
                pg.op(act, lambda e: e.copy(st_h[:, j, h, :], hb_ap[:, 3 + T:4 + T]),
                      reads=[b_hbuf[hbi][s]], writes=[b_st_h[j][h]])
            pg.op(dve, lambda e: e.scalar_tensor_tensor(y_sb[:, h, sl], hb_ap[:, off:off + NS], 0.25, d["sg"],
                                                        op0=ALU.mult, op1=ALU.mult),
                  reads=[b_hbuf[hbi][s], d["bsg"]], writes=[b_y[h][s]])
            Pool.rel(d["bsg"], d["btr"], d["bti"], d["ba"])

        groups = [list(range(g0, min(g0 + c.GA, H))) for g0 in range(0, H, c.GA)]
        done_units = set()
        out_done = set()

        def try_out():
            for gi, g in enumerate(groups):
                if gi in out_done:
                    continue
                if all((hd * NSUB + s) in done_units for hd in g for s in range(NSUB)):
                    out_proj(g)
                    out_done.add(gi)

        LAG_E1, LAG_PE2, LAG_E3, LAG_E4, LAG_E5 = 1, 2, 3, 4, 6
        for i in range(NU + LAG_E5 + 1):
            if 0 <= i - LAG_E5 < NU:
                E5(i - LAG_E5)
                done_units.add(i - LAG_E5)
                try_out()
            if i < NU:
                PE0(i)
            if 0 <= i - LAG_E1 < NU:
                E1(i - LAG_E1)
            if 0 <= i - LAG_E3 < NU:
                E3(i - LAG_E3)
            if 0 <= i - LAG_PE2 < NU:
                PE2(i - LAG_PE2)
            if i % 2 == 0:
                for u in (i - LAG_E4 - 1, i - LAG_E4):
                    if 0 <= u < NU:
                        E4(u)
            if i == 1:
                load_wout(L)
            if i + 1 < NU:
                ws.prefetch(units[min(i + 2, NU - 1)][0])
        assert len(out_done) == len(groups)

    def layer_B(li, j, ti):
        L = f"B{j}"
        units = [(h, s) for h in range(H) for s in range(NSUB)]
        NU = len(units)
        pp_in = PairPool([(0, 1), (2, 3), (4, 5)])
        ws = WStream(L)
        widx = ws.widx
        U = [dict() for _ in range(NU)]
        ws.prefetch(0)

        def col(base, h):
            return pv[:, base + h: base + h + 1]

        def PE0(u, half):
            h, s = units[u]
            assert h in widx, f"weights for head {h} not loaded"
            d = U[u]
            pr = pp_in.get(("B", u, half))
            d["in%d" % half] = pr
            wt = w_sb[widx[h]]
            sl = slice(s * NS, (s + 1) * NS)
            parts = (1, 2) if half == 0 else (0, 3)
            for n, part in enumerate(parts):
                bank = pr[n]
                fns = []
                for k in range(KC):
                    o = (part * KC + k) * P
                    fns.append(lambda e, o=o, k=k, bank=bank: e.matmul(
                        bank_ap(bank), wt[:, o:o + P], h_sb[:, k, sl], start=(k == 0), stop=(k == KC - 1)))
                pg.ops(pe, fns, reads=[b_w[widx[h]], b_h[s]], writes=[banks[bank]])
            if half == 1 and s == NSUB - 1:
                ws.release(h)

        def E1a(u):
            h, s = units[u]
            d = U[u]
            hbi = h % NHB
            bc_, bv_ = d["in0"]
            pb_ap = xbuf[hbi]
            off = 4 + s * NS
            xv, bxv = p_xv.get(("B", u))
            pg.op(act, lambda e: e.activation(xv, bank_ap(bv_), AF.Identity, bias=cst[:, 2:3], scale=1.0),
                  reads=[banks[bv_], b_const], writes=[bxv])
            if s == 0:
                pg.op(act, lambda e: e.copy(pb_ap[:, 2:4], st_pb[:, j, h, :]),
                      reads=[b_st_pb[j][h]], writes=[b_xbuf[hbi][0]])
            pg.op(dve, lambda e: e.tensor_tensor(pb_ap[:, off:off + NS], bank_ap(bc_), xv, op=ALU.mult),
                  reads=[banks[bc_], bxv], writes=[b_xbuf[hbi][s]])
            Pool.rel(bxv)
            for b in d["in0"]:
                banks[b].owner = None
            if s == NSUB - 1:
                pg.op(act, lambda e: e.copy(st_pb[:, j, h, :], pb_ap[:, 2 + T:4 + T]),
                      reads=[b_xbuf[hbi][s]], writes=[b_st_pb[j][h]])
            cc, bcc = p_cc.get(("B", u))
            rd = [b_xbuf[hbi][s]] + ([b_xbuf[hbi][s - 1]] if s > 0 else []) + [b_pv]
            cw = lambda k: col(c.o_bcw + (j * 3 + k) * H, h)
            pg.op(dve, lambda e: e.tensor_scalar(cc, pb_ap[:, off:off + NS], cw(2), None, op0=ALU.mult),
                  reads=rd, writes=[bcc])
            for k in (1, 0):
                sh = 2 - k
                pg.op(dve, lambda e, k=k, sh=sh: e.scalar_tensor_tensor(
                    cc, pb_ap[:, off - sh:off - sh + NS], cw(k), cc, op0=ALU.mult, op1=ALU.add),
                    reads=rd + [bcc], writes=[bcc])
            d.update(cc=cc, bcc=bcc)

        def E1b(u):
            h, s = units[u]
            d = U[u]
            bb_, bg_ = d["in1"]
            sl = slice(s * NS, (s + 1) * NS)
            tg, btg = p_tg.get(("B", u))
            pg.op(act, lambda e: e.activation(tg, bank_ap(bg_), AF.Tanh, bias=cst[:, 2:3], scale=0.5),
                  reads=[banks[bg_], b_const], writes=[btg])
            sg, bsg = p_sg.get(("B", u))
            pg.op(dve, lambda e: e.scalar_tensor_tensor(sg, tg, 1.0, bank_ap(bg_), op0=ALU.add, op1=ALU.mult),
                  reads=[btg, banks[bg_]], writes=[bsg])
            Pool.rel(btg)
            pg.op(dve, lambda e: e.tensor_tensor(d["cc"], d["cc"], bank_ap(bb_), op=ALU.mult),
                  reads=[d["bcc"], banks[bb_]], writes=[d["bcc"]])
            for b in d["in1"]:
                banks[b].owner = None
            pg.op(dve, lambda e: e.scalar_tensor_tensor(y_sb[:, h, sl], d["cc"], 0.5, sg, op0=ALU.mult, op1=ALU.mult),
                  reads=[d["bcc"], bsg], writes=[b_y[h][s]])
            Pool.rel(bsg, d["bcc"])

        groups = [list(range(g0, min(g0 + c.GB, H))) for g0 in range(0, H, c.GB)]
        done_units = set()
        out_done = set()

        def try_out():
            for gi, g in enumerate(groups):
                if gi in out_done:
                    continue
                if all((hd * NSUB + s) in done_units for hd in g for s in range(NSUB)):
                    out_proj(g)
                    out_done.add(gi)

        for i in range(NU + 1):
            if i < NU:
                PE0(i, 0)
            if i >= 1:
                E1b(i - 1)
                done_units.add(i - 1)
            if i < NU:
                PE0(i, 1)
                E1a(i)
            if i >= 1:
                try_out()
            if i == 1:
                load_wout(L)
            if i + 1 < NU:
                ws.prefetch(units[min(i + 2, NU - 1)][0])
        assert len(out_done) == len(groups)

    for ti in range(NT):
        t0 = ti * T
        for k in range(KC):
            pg.dma(sp, s_xin, x_sb[:, k, :], xT_v[:, k, t0:t0 + T], writes=[b_x[k][s] for s in range(NSUB)])
        evx = (s_xin, pg.dma_counts[s_xin])
        for k in range(KC):
            for s in range(NSUB):
                b_x[k][s].w = evx
        for li_pos, L in enumerate(c.layers):
            kind, j = L[0], int(L[1:])
            li = 2 * j + (0 if kind == "A" else 1)
            layer_norm_in(li)
            if kind == "A":
                layer_A(li, j, ti)
            else:
                layer_B(li, j, ti)
        if c.final_norm:
            for s in range(NSUB):
                rmsnorm_stats(s, 6 + (s % 2))
            for s in range(NSUB):
                sl = slice(s * NS, (s + 1) * NS)
                for k in range(KC):
                    g = pv[:, c.o_fg + k: c.o_fg + k + 1]
                    pg.op(dve, lambda e, k=k, g=g: e.scalar_tensor_tensor(
                        x_sb[:, k, sl], x_sb[:, k, sl], g, rs_sb[s][:, :], op0=ALU.mult, op1=ALU.mult),
                        reads=[b_x[k][s], b_rs[s], b_pv], writes=[b_x[k][s]])
        for k in range(KC):
            pg.dma(sp, s_out[k], oT_v[:, k, t0:t0 + T], x_sb[:, k, :], reads=[b_x[k][s] for s in range(NSUB)])

    for k in range(KC):
        sp.h.wait_ge(s_out[k], pg.dma_counts[s_out[k]])
    return nc, pg


def pack_params(c: Cfg, norm_g, final_g, a_conv_w, a_conv_b, a_gate_b, a_lambda, b_conv_w):
    H, KC = c.H, c.KC
    cols = [
        norm_g.reshape(c.DEPTH, KC, P).transpose(2, 0, 1).reshape(P, c.DEPTH * KC),
        final_g.reshape(KC, P).transpose(1, 0),
        a_conv_w.reshape(c.NA, 4, H, P).transpose(3, 0, 1, 2).reshape(P, c.NA * 4 * H),
        a_conv_b.reshape(c.NA, H, P).transpose(2, 0, 1).reshape(P, c.NA * H),
        a_gate_b.reshape(c.NA, H, 2, P).transpose(3, 0, 2, 1).reshape(P, c.NA * 2 * H),
        a_lambda.reshape(c.NA, H, P).transpose(2, 0, 1).reshape(P, c.NA * H),
        b_conv_w.reshape(c.NB, 3, H, P).transpose(3, 0, 1, 2).reshape(P, c.NB * 3 * H),
    ]
    pv = np.ascontiguousarray(np.concatenate(cols, axis=1), dtype=np.float32)
    assert pv.shape == (P, c.NPV)
    return pv


def pack_weights(c: Cfg, a_w_in, a_gate_w, a_w_out, b_w_in, b_w_out):
    H, KC, D = c.H, c.KC, c.D
    wa = np.empty((c.NA, H, P, c.CA), np.float32)
    for j in range(c.NA):
        win = a_w_in[j].reshape(KC, P, 2, H, P).transpose(3, 1, 2, 0, 4).reshape(H, P, 2 * KC * P)
        wa[j, :, :, :2 * KC * P] = win
        wa[j, :, :, 2 * KC * P:2 * KC * P + 2 * P] = a_gate_w[j]
        wa[j, :, :, 2 * KC * P + 2 * P:] = a_w_out[j].reshape(H, P, D)
    wb = np.empty((c.NB, H, P, c.CB), np.float32)
    for j in range(c.NB):
        win = b_w_in[j].reshape(KC, P, 4, H, P).transpose(3, 1, 2, 0, 4).reshape(H, P, 4 * KC * P)
        wb[j, :, :, :4 * KC * P] = win
        wb[j, :, :, 4 * KC * P:] = b_w_out[j].reshape(H, P, D)
    return wa, wb


_CACHE = {}


def _get_prog(key, cfg):
    if key not in _CACHE:
        _CACHE[key] = build_program(cfg)[0]
    return _CACHE[key]


FUSED = True


def kernel(x, norm_g, a_w_in, a_conv_w, a_conv_b, a_gate_w, a_gate_b, a_lambda,
           a_w_out, b_w_in, b_conv_w, b_w_out, final_g):
    x = np.asarray(x, np.float32)
    B, S, D = x.shape
    base = Cfg()
    pv = pack_params(base, *(np.asarray(v, np.float32) for v in
                             (norm_g, final_g, a_conv_w, a_conv_b, a_gate_b, a_lambda, b_conv_w)))
    wa, wb = pack_weights(base, *(np.asarray(v, np.float32) for v in (a_w_in, a_gate_w, a_w_out, b_w_in, b_w_out)))
    xTs = [np.ascontiguousarray(x[b].T) for b in range(B)]
    if FUSED:
        stages = [(("A0", "B0", "A1", "B1"), True)]
    else:
        stages = [(("A0",), False), (("B0",), False), (("A1",), False), (("B1",), True)]
    cur = xTs
    for layers, fin in stages:
        cfg = Cfg(layers=layers, final_norm=fin)
        nc = _get_prog((layers, fin), cfg)
        in_maps = [{"xT": cur[b], "pv": pv, "wa": wa, "wb": wb} for b in range(B)]
        res = run_bass_kernel_spmd(nc, in_maps, core_ids=list(range(B)))
        cur = [np.ascontiguousarray(res.results[b]["oT"]) for b in range(B)]
    out = np.stack([cur[b].T for b in range(B)], axis=0)
    return np.ascontiguousarray(out, dtype=np.float32)
```
